# Optimizing a Trainium2 kernel written in Bass

```python
import jax, jax.numpy as jnp
from jax import lax
import numpy as np

D_MODEL = 4096
BATCH = 2
SEQ = 4096
DEPTH = 2

N_MIXERS = 2
N_POOL_LAYERS = (DEPTH + 1) // 2
N_RWKV_LAYERS = DEPTH // 2
POOL_WINDOWS = (2, 4, 8, 16)
N_POOL_GROUPS = len(POOL_WINDOWS)
POOL_GROUP = D_MODEL // N_POOL_GROUPS
RWKV_HEAD = 64
RWKV_HEADS = D_MODEL // RWKV_HEAD


def _lora_dim(factor, power):
    return max(32, int(round(factor * D_MODEL ** power / 32)) * 32)


DECAY_LORA = _lora_dim(1.8, 0.5)
AAA_LORA = _lora_dim(1.8, 0.5)
GATE_LORA = _lora_dim(0.6, 0.8)
N_SHIFT_MIX = 6
LNX_EPS = 64e-5
D_FF = ((8 * D_MODEL // 3 + 255) // 256) * 256
FFN_CONV = 3
N_ADA = 6
NORM_EPS = 1e-6

kernel_name = "hybrid_pool_rwkv7_convglu_adaln"


def rmsnorm(x, g):
    xf = x.astype(jnp.float32)
    xf = xf * lax.rsqrt(jnp.mean(xf * xf, axis=-1, keepdims=True) + NORM_EPS)
    return (xf * g.astype(jnp.float32)).astype(x.dtype)


def pool_mixer(h, w_grp, scale):
    B, T, D = h.shape
    cs = jnp.cumsum(h.astype(jnp.float32), axis=1)
    pos = jnp.arange(1, T + 1, dtype=jnp.float32)
    outs = []
    for gi, win in enumerate(POOL_WINDOWS):
        csg = cs[..., gi * POOL_GROUP:(gi + 1) * POOL_GROUP]
        lag = jnp.pad(csg, ((0, 0), (win, 0), (0, 0)))[:, :T]
        cnt = jnp.minimum(pos, float(win))[None, :, None]
        outs.append((csg - lag) / cnt)
    pooled = jnp.concatenate(outs, axis=-1).astype(h.dtype) - h
    y = jnp.einsum('btgc,gce->btge', pooled.reshape(B, T, N_POOL_GROUPS, POOL_GROUP), w_grp)
    return y.reshape(B, T, D) * scale


def wkv7_scan(r, decay, k, v, a, b):
    B, T, H, N = r.shape

    def step(S, inp):
        r_t, w_t, k_t, v_t, a_t, b_t = inp
        sa = jnp.einsum('bhvk,bhk->bhv', S, a_t)
        S = (S * w_t[:, :, None, :] + sa[..., None] * b_t[:, :, None, :]
             + v_t[..., None] * k_t[:, :, None, :])
        return S, jnp.einsum('bhvk,bhk->bhv', S, r_t)

    seq = tuple(jnp.moveaxis(z, 1, 0) for z in (r, decay, k, v, a, b))
    S0 = jnp.zeros((B, H, N, N), jnp.float32)
    _, y = lax.scan(step, S0, seq)
    return jnp.moveaxis(y, 0, 1)


def rwkv7_time_mix(h, mu, wr, wk, wv, w0, w1, w2, a0, a1, a2, g1, g2,
                   k_k, k_a, r_k, lnx_g, lnx_b, wo):
    B, T, D = h.shape
    H, N = RWKV_HEADS, RWKV_HEAD
    f32 = jnp.float32
    dx = jnp.pad(h, ((0, 0), (1, 0), (0, 0)))[:, :T] - h
    xr, xw, xk, xv, xa, xg = [h + dx * mu[n] for n in range(N_SHIFT_MIX)]
    r = xr @ wr
    k = xk @ wk
    v = xv @ wv
    w = -jax.nn.softplus(-(w0 + jnp.tanh(xw @ w1) @ w2).astype(f32)) - 0.5
    a = jax.nn.sigmoid((a0 + (xa @ a1) @ a2).astype(f32))
    g = jax.nn.sigmoid(xg @ g1) @ g2
    kk = (k * k_k).astype(f32).reshape(B, T, H, N)
    kk = kk / jnp.maximum(jnp.linalg.norm(kk, axis=-1, keepdims=True), 1e-12)
    k = k.astype(f32) * (1.0 + (a - 1.0) * k_a.astype(f32))
    rh = r.astype(f32).reshape(B, T, H, N)
    kh = k.reshape(B, T, H, N)
    vh = v.astype(f32).reshape(B, T, H, N)
    ah = a.reshape(B, T, H, N)
    decay = jnp.exp(-jnp.exp(w)).reshape(B, T, H, N)
    y = wkv7_scan(rh, decay, kh, vh, -kk, kk * ah)
    mean = jnp.mean(y, axis=-1, keepdims=True)
    var = jnp.mean(jnp.square(y - mean), axis=-1, keepdims=True)
    y = ((y - mean) * lax.rsqrt(var + LNX_EPS)).reshape(B, T, D)
    y = y * lnx_g.astype(f32) + lnx_b.astype(f32)
    bonus = jnp.sum(rh * kh * r_k.astype(f32), axis=-1, keepdims=True) * vh
    y = (y + bonus.reshape(B, T, D)).astype(h.dtype)
    return (y * g) @ wo


def conv_glu(h, w_up, conv_w, conv_b, w_down):
    gate, val = jnp.split(h @ w_up, 2, axis=-1)
    gate = lax.conv_general_dilated(
        gate, conv_w[:, None, :], window_strides=(1,),
        padding=[(FFN_CONV - 1, 0)], dimension_numbers=('NWC', 'WIO', 'NWC'),
        feature_group_count=gate.shape[-1]) + conv_b
    return (jax.nn.gelu(gate, approximate=False) * val) @ w_down


def setup_inputs(seed: int = 0) -> dict:
    key = jax.random.key(seed)
    ks = iter(jax.random.split(key, 32))
    D, F, H, N = D_MODEL, D_FF, RWKV_HEADS, RWKV_HEAD
    LP, LR = N_POOL_LAYERS, N_RWKV_LAYERS
    f32 = jnp.float32

    def nrm(shape, fan_in, s=1.0):
        return jax.random.normal(next(ks), shape, f32) * (s * fan_in ** -0.5)

    def gain(shape, s=0.05):
        return 1.0 + s * jax.random.normal(next(ks), shape, f32)

    def small(shape, s=0.02):
        return s * jax.random.normal(next(ks), shape, f32)

    def unif(shape, lo, hi):
        return jax.random.uniform(next(ks), shape, f32, lo, hi)

    return {
        "x": jax.random.normal(next(ks), (BATCH, SEQ, D), f32),
        "c": jax.random.normal(next(ks), (BATCH, D), f32),
        "ada_w": nrm((DEPTH, D, N_ADA * D), D, 0.5),
        "ada_b": small((DEPTH, N_ADA * D)),
        "norm_g": gain((DEPTH, 2, D)),
        "pool_w": nrm((LP, N_POOL_GROUPS, POOL_GROUP, POOL_GROUP), POOL_GROUP),
        "pool_scale": gain((LP, D), 0.1),
        "rwkv_mu": unif((LR, N_SHIFT_MIX, D), 0.0, 1.0),
        "rwkv_wr": nrm((LR, D, D), D),
        "rwkv_wk": nrm((LR, D, D), D),
        "rwkv_wv": nrm((LR, D, D), D),
        "rwkv_w0": unif((LR, D), -6.5, -1.5),
        "rwkv_w1": nrm((LR, D, DECAY_LORA), D),
        "rwkv_w2": nrm((LR, DECAY_LORA, D), DECAY_LORA, 0.1),
        "rwkv_a0": small((LR, D), 0.1),
        "rwkv_a1": nrm((LR, D, AAA_LORA), D),
        "rwkv_a2": nrm((LR, AAA_LORA, D), AAA_LORA, 0.1),
        "rwkv_g1": nrm((LR, D, GATE_LORA), D),
        "rwkv_g2": nrm((LR, GATE_LORA, D), GATE_LORA),
        "rwkv_kk": 0.85 + 0.05 * jax.random.normal(next(ks), (LR, D), f32),
        "rwkv_ka": gain((LR, D)),
        "rwkv_rk": small((LR, H, N), 0.1),
        "rwkv_lnx_g": gain((LR, D)),
        "rwkv_lnx_b": small((LR, D)),
        "rwkv_wo": nrm((LR, D, D), D),
        "ffn_w_up": nrm((DEPTH, D, 2 * F), D),
        "ffn_conv_w": nrm((DEPTH, FFN_CONV, F), FFN_CONV),
        "ffn_conv_b": small((DEPTH, F)),
        "ffn_w_down": nrm((DEPTH, F, D), F),
        "final_g": gain((D,)),
    }


def reference(x, c, ada_w, ada_b, norm_g, pool_w, pool_scale, rwkv_mu, rwkv_wr,
              rwkv_wk, rwkv_wv, rwkv_w0, rwkv_w1, rwkv_w2, rwkv_a0, rwkv_a1,
              rwkv_a2, rwkv_g1, rwkv_g2, rwkv_kk, rwkv_ka, rwkv_rk, rwkv_lnx_g,
              rwkv_lnx_b, rwkv_wo, ffn_w_up, ffn_conv_w, ffn_conv_b, ffn_w_down,
              final_g):
    c_act = jax.nn.silu(c)
    for i in range(DEPTH):
        mod = c_act @ ada_w[i] + ada_b[i]
        sh1, sc1, gt1, sh2, sc2, gt2 = [m[:, None, :] for m in jnp.split(mod, N_ADA, axis=-1)]
        h = rmsnorm(x, norm_g[i, 0]) * (1.0 + sc1) + sh1
        j = i // N_MIXERS
        if i % N_MIXERS == 0:
            y = pool_mixer(h, pool_w[j], pool_scale[j])
        else:
            y = rwkv7_time_mix(h, rwkv_mu[j], rwkv_wr[j], rwkv_wk[j], rwkv_wv[j],
                               rwkv_w0[j], rwkv_w1[j], rwkv_w2[j], rwkv_a0[j],
                               rwkv_a1[j], rwkv_a2[j], rwkv_g1[j], rwkv_g2[j],
                               rwkv_kk[j], rwkv_ka[j], rwkv_rk[j], rwkv_lnx_g[j],
                               rwkv_lnx_b[j], rwkv_wo[j])
        x = x + gt1 * y
        h = rmsnorm(x, norm_g[i, 1]) * (1.0 + sc2) + sh2
        x = x + gt2 * conv_glu(h, ffn_w_up[i], ffn_conv_w[i], ffn_conv_b[i], ffn_w_down[i])
    return rmsnorm(x, final_g)
```

```python
import numpy as np
import concourse.bass as bass
import concourse.mybir as mybir
from concourse.bass_utils import run_bass_kernel_spmd
from contextlib import ExitStack

F32 = mybir.dt.float32
BF16 = mybir.dt.bfloat16
AF = mybir.ActivationFunctionType
ALU = mybir.AluOpType
AX = mybir.AxisListType

D = 4096
NKC = 32
FF = 11008
NFT = 86
B = 2
T = 4096
NCORE = 8
TT = 512
NORM_EPS = 1e-6
LNX_EPS = 64e-5

ENGS = ("pe", "act", "dve", "pool", "sp")
SEM_ROT = 30000
DMA_POOL = 6


class Buf:
    __slots__ = ("name", "w", "r")

    def __init__(self, name=""):
        self.name = name
        self.w = None
        self.r = []


class Op:
    __slots__ = ("eng", "fn", "deps", "idx", "need", "sem", "val", "dma", "waits")


class Prog:
    def __init__(self, nc):
        self.nc = nc
        self.ops = {e: [] for e in ENGS}
        self.all = []
        self.stack = ExitStack()
        self.dmas = {e: [] for e in ENGS}
        self.nm = 0

    def sb(self, shape, dt, name=None):
        self.nm += 1
        return self.stack.enter_context(self.nc.sbuf_tensor(name or f"sb{self.nm}", list(shape), dt))

    def ps(self, shape, dt=F32, name=None):
        self.nm += 1
        return self.stack.enter_context(self.nc.psum_tensor(name or f"ps{self.nm}", list(shape), dt))

    def op(self, eng, fn, reads=(), writes=(), dma=False):
        o = Op()
        o.eng = eng
        o.fn = fn
        o.dma = dma
        o.need = False
        o.sem = None
        o.val = 0
        o.waits = []
        deps = []
        for b in reads:
            if b.w is not None:
                deps.append(b.w)
        for b in writes:
            if b.w is not None:
                deps.append(b.w)
            deps.extend(b.r)
        for b in reads:
            b.r.append(o)
        for b in writes:
            b.w = o
            b.r = []
        o.deps = [d for d in deps if not (eng == "pe" and d.eng == "pe" and not d.dma) and d is not o]
        o.idx = len(self.ops[eng])
        self.ops[eng].append(o)
        self.all.append(o)
        if dma:
            lst = self.dmas[eng]
            o.need = True
            if len(lst) >= DMA_POOL:
                o.deps.append(lst[len(lst) - DMA_POOL])
            lst.append(o)
        return o

    def dma(self, eng, out, in_, reads=(), writes=()):
        return self.op(eng, lambda e: e.dma_start(out=out, in_=in_), reads, writes, dma=True)

    def finish(self, out_ops):
        o = self.op("sp", lambda e: e.nop(), (), ())
        o.deps.extend(out_ops)
        return o

    def emit(self):
        nc = self.nc
        st = self.stack
        for o in self.all:
            for d in o.deps:
                d.need = True
        nsem = [0]

        def newsem(tag):
            nsem[0] += 1
            return st.enter_context(nc.semaphore(f"s_{tag}_{nsem[0]}"))

        for e in ENGS:
            cur = None
            cnt = 0
            dsems = None
            k = 0
            for o in self.ops[e]:
                if o.dma:
                    if dsems is None:
                        dsems = [newsem(f"d{e}") for _ in range(DMA_POOL)]
                    o.sem = dsems[k % DMA_POOL]
                    o.val = 16 * (k // DMA_POOL + 1)
                    k += 1
                elif o.need:
                    if cur is None or cnt >= SEM_ROT:
                        cur = newsem(e)
                        cnt = 0
                    cnt += 1
                    o.sem = cur
                    o.val = cnt
        for e in ENGS:
            seen = {}
            for o in self.ops[e]:
                best = {}
                for d in o.deps:
                    key = id(d.sem)
                    if key not in best or best[key][1] < d.val:
                        best[key] = (d.sem, d.val)
                for key, (s, v) in best.items():
                    if seen.get(key, 0) >= v:
                        continue
                    seen[key] = v
                    o.waits.append((s, v))
        with nc.Block() as block:
            def mk(e):
                def body(eng):
                    for o in self.ops[e]:
                        for (s, v) in o.waits:
                            eng.wait_ge(s, v)
                        ins = o.fn(eng)
                        if o.need:
                            ins.then_inc(o.sem, 16 if o.dma else 1)
                return body
            if self.ops["sp"]:
                block.sync(mk("sp"))
            if self.ops["pe"]:
                block.tensor(mk("pe"))
            if self.ops["act"]:
                block.scalar(mk("act"))
            if self.ops["dve"]:
                block.vector(mk("dve"))
            if self.ops["pool"]:
                block.gpsimd(mk("pool"))
        self.stack.close()


class Ring:
    def __init__(self, P, n, shape, dt, name, tensors=None, bufs=None):
        self.t = tensors if tensors is not None else [P.sb(shape, dt, f"{name}{i}") for i in range(n)]
        self.b = bufs if bufs is not None else [Buf(f"{name}{i}") for i in range(n)]
        self.i = 0
        self.n = n

    def next(self):
        k = self.i % self.n
        self.i += 1
        return self.t[k], self.b[k]


class Ctx:
    pass


def col_pieces(n):
    out = []
    a = 0
    while a < n:
        b = min(n, a + 512)
        out.append((a, b))
        a = b
    return out


def setup_common(P, C, W=532, ps_ss=None):
    C.ones = P.sb([128, 128], F32, "ones")
    C.Bones = Buf("ones")
    P.op("dve", lambda e: e.memset(C.ones[:], 1.0), (), [C.Bones])
    C.eps = P.sb([128, 1], F32, "epsn")
    C.Beps = Buf("eps")
    P.op("dve", lambda e: e.memset(C.eps[:], NORM_EPS), (), [C.Beps])
    C.xring = Ring(P, 3, [128, W], F32, "xr")
    C.sqring = Ring(P, 2, [128, W], F32, "sq")
    C.tring = Ring(P, 2, [128, W], F32, "tr")
    C.rstd = P.sb([128, W], F32, "rstd")
    C.Brstd = Buf("rstd")
    if ps_ss is None:
        C.ps_ss = [P.ps([128, 512], F32, f"ps_ss{i}") for i in range(2)]
        C.Bps_ss = [Buf(f"ps_ss{i}") for i in range(2)]
    else:
        C.ps_ss, C.Bps_ss = ps_ss


def rms_stats(P, C, xT, n, Bx):
    pcs = col_pieces(n)
    for kc in range(NKC):
        xt, bx = C.xring.next()
        P.dma("sp", xt[:, 0:n], xT[kc * 128:(kc + 1) * 128, 0:n], reads=[Bx[kc]], writes=[bx])
        sq, bs = C.sqring.next()
        P.op("act", lambda e, sq=sq, xt=xt: e.activation(out=sq[:, 0:n], in_=xt[:, 0:n], func=AF.Square),
             [bx], [bs])
        for i, (a, b) in enumerate(pcs):
            P.op("pe", lambda e, i=i, a=a, b=b, sq=sq, kc=kc: e.matmul(
                C.ps_ss[i][:, 0:b - a], C.ones[:], sq[:, a:b], start=(kc == 0), stop=(kc == NKC - 1)),
                [bs, C.Bones], [C.Bps_ss[i]])
    for i, (a, b) in enumerate(pcs):
        P.op("act", lambda e, i=i, a=a, b=b: e.activation(
            out=C.rstd[:, a:b], in_=C.ps_ss[i][:, 0:b - a], func=AF.Sqrt, scale=1.0 / D, bias=C.eps[:]),
            [C.Bps_ss[i], C.Beps], [C.Brstd])
    P.op("dve", lambda e: e.reciprocal(out=C.rstd[:, 0:n], in_=C.rstd[:, 0:n]), [C.Brstd], [C.Brstd])


def norm_apply(P, C, xT, n, Bx, scale_ap, bias_ap, Bsc, sink, post=None):
    for kc in range(NKC):
        xt, bx = C.xring.next()
        P.dma("sp", xt[:, 0:n], xT[kc * 128:(kc + 1) * 128, 0:n], reads=[Bx[kc]], writes=[bx])
        tt, bt = C.tring.next()
        P.op("dve", lambda e, tt=tt, xt=xt: e.tensor_tensor(out=tt[:, 0:n], in0=xt[:, 0:n], in1=C.rstd[:, 0:n],
                                                           op=ALU.mult), [bx, C.Brstd], [bt])
        out_ap, bout = sink(kc)
        if bias_ap is None:
            P.op("act", lambda e, tt=tt, out_ap=out_ap, kc=kc: e.activation(
                out=out_ap, in_=tt[:, 0:n], func=AF.Identity, scale=scale_ap[:, kc:kc + 1], bias=0.0),
                [bt, Bsc], [bout])
        else:
            P.op("act", lambda e, tt=tt, out_ap=out_ap, kc=kc: e.activation(
                out=out_ap, in_=tt[:, 0:n], func=AF.Identity, scale=scale_ap[:, kc:kc + 1],
                bias=bias_ap[:, kc:kc + 1]), [bt, Bsc], [bout])
        if post is not None:
            post(kc)


def load_mods(P, C, mod_ap, ng_ap, extra_scale_ap=None):
    C.mod = P.sb([128, 6, NKC], F32, "mod_sb")
    C.ng = P.sb([128, 2, NKC], F32, "ng_sb")
    C.A = P.sb([128, 2, NKC], F32, "A")
    C.Bmod = Buf("mod")
    P.dma("sp", C.mod[:], mod_ap, writes=[C.Bmod])
    P.dma("sp", C.ng[:], ng_ap, writes=[C.Bmod])
    for s in range(2):
        P.op("dve", lambda e, s=s: e.scalar_tensor_tensor(
            out=C.A[:, s, :], in0=C.mod[:, 3 * s + 1, :], scalar=1.0, in1=C.ng[:, s, :], op0=ALU.add, op1=ALU.mult),
            [C.Bmod], [C.Bmod])


def ffn_tile(P, C, l, w_up, conv_w, conv_b, w_down, x1T, Bx1, x2T, Bx2, halo_valid):
    n = TT
    for j in range(NFT):
        wg, bwg = C.wuring.next()
        wv, bwv = C.wuring.next()
        P.dma("pool", wg[:], w_up[:, j * 128:(j + 1) * 128].rearrange("(kc p) c -> p kc c", p=128),
              writes=[bwg])
        P.dma("pool", wv[:],
              w_up[:, FF + j * 128:FF + (j + 1) * 128].rearrange("(kc p) c -> p kc c", p=128), writes=[bwv])
        pg, bpg = C.pgring.next()
        pv, bpv = C.pvring.next()
        ph, bph = C.phring.next()
        for kc in range(NKC):
            P.op("pe", lambda e, kc=kc, pg=pg, wg=wg: e.matmul(pg[:, 0:n], wg[:, kc, :], C.hT[:, kc, 2:2 + n],
                                                              start=(kc == 0), stop=(kc == NKC - 1)),
                 [bwg, C.BhT], [bpg])
        for kc in range(NKC):
            P.op("pe", lambda e, kc=kc, ph=ph, wg=wg: e.matmul(ph[:, 0:2], wg[:, kc, :], C.hT[:, kc, 0:2],
                                                              start=(kc == 0), stop=(kc == NKC - 1)),
                 [bwg, C.BhT], [bph])
        for kc in range(NKC):
            P.op("pe", lambda e, kc=kc, pv=pv, wv=wv: e.matmul(pv[:, 0:n], wv[:, kc, :], C.hT[:, kc, 2:2 + n],
                                                              start=(kc == 0), stop=(kc == NKC - 1)),
                 [bwv, C.BhT], [bpv])
        t1, bt1 = C.cring.next()
        hh, bhh = C.hring.next()
        P.op("dve", lambda e, hh=hh, ph=ph: e.tensor_tensor(out=hh[:, 0:2], in0=ph[:, 0:2], in1=halo_valid[:, 0:2],
                                                           op=ALU.mult), [bph, C.Bconst], [bhh])
        P.op("act", lambda e, t1=t1, pg=pg, j=j: e.activation(
            out=t1[:, 0:n], in_=pg[:, 0:n], func=AF.Identity, scale=C.cw[:, 2, j:j + 1], bias=C.cb[:, j:j + 1]),
            [bpg, C.Bcw], [bt1])
        P.op("dve", lambda e, t1=t1, pg=pg, j=j: e.scalar_tensor_tensor(
            out=t1[:, 2:n], in0=pg[:, 0:n - 2], scalar=C.cw[:, 0, j:j + 1], in1=t1[:, 2:n], op0=ALU.mult,
            op1=ALU.add), [bpg, C.Bcw, bt1], [bt1])
        P.op("dve", lambda e, t1=t1, hh=hh, j=j: e.scalar_tensor_tensor(
            out=t1[:, 0:2], in0=hh[:, 0:2], scalar=C.cw[:, 0, j:j + 1], in1=t1[:, 0:2], op0=ALU.mult,
            op1=ALU.add), [bhh, C.Bcw, bt1], [bt1])
        P.op("dve", lambda e, t1=t1, pg=pg, j=j: e.scalar_tensor_tensor(
            out=t1[:, 1:n], in0=pg[:, 0:n - 1], scalar=C.cw[:, 1, j:j + 1], in1=t1[:, 1:n], op0=ALU.mult,
            op1=ALU.add), [bpg, C.Bcw, bt1], [bt1])
        P.op("dve", lambda e, t1=t1, hh=hh, j=j: e.scalar_tensor_tensor(
            out=t1[:, 0:1], in0=hh[:, 1:2], scalar=C.cw[:, 1, j:j + 1], in1=t1[:, 0:1], op0=ALU.mult,
            op1=ALU.add), [bhh, C.Bcw, bt1], [bt1])
        P.op("act", lambda e, t1=t1: e.activation(out=t1[:, 0:n], in_=t1[:, 0:n], func=AF.Gelu), [bt1], [bt1])
        P.op("dve", lambda e, t1=t1, pv=pv, j=j: e.tensor_tensor(out=C.act[:, j, :], in0=t1[:, 0:n], in1=pv[:, 0:n],
                                                                op=ALU.mult), [bt1, bpv], [C.Bact[j]])
    for i in range(NKC):
        po, bpo = C.poring.next()
        HF = NFT // 2
        for hf in range(2):
            wd, bwd = C.wdring.next()
            P.dma("pool", wd[:], w_down[hf * HF * 128:(hf + 1) * HF * 128, i * 128:(i + 1) * 128].rearrange(
                "(j p) c -> p j c", p=128), writes=[bwd])
            for jj in range(HF):
                j = hf * HF + jj
                P.op("pe", lambda e, j=j, jj=jj, po=po, wd=wd: e.matmul(po[:, 0:n], wd[:, jj, :], C.act[:, j, :],
                                                                       start=(j == 0), stop=(j == NFT - 1)),
                     [bwd, C.Bact[j]], [bpo])
        xt, bx = C.xring.next()
        P.dma("sp", xt[:, 0:n], x1T[i * 128:(i + 1) * 128, 2:2 + n], reads=[Bx1[i]], writes=[bx])
        P.op("dve", lambda e, xt=xt, po=po, i=i: e.scalar_tensor_tensor(
            out=xt[:, 0:n], in0=po[:, 0:n], scalar=C.mod[:, 5, i:i + 1], in1=xt[:, 0:n], op0=ALU.mult, op1=ALU.add),
            [bpo, bx, C.Bmod], [bx])
        P.dma("sp", x2T[i * 128:(i + 1) * 128, 0:n], xt[:, 0:n], reads=[bx], writes=[Bx2[i]])


def setup_ffn(P, C):
    C.hT = P.sb([128, NKC, 2 + TT], BF16, "hT")
    C.BhT = Buf("hT")
    C.actflat = P.sb([128, NFT * TT], BF16, "act")
    C.act = C.actflat[:, :].rearrange("p (j n) -> p j n", n=TT)
    C.Bact = [Buf(f"act{j}") for j in range(NFT)]
    C.wuring = Ring(P, 3, [128, NKC, 128], BF16, "wu")
    C.wdring = Ring(P, 2, [128, NFT // 2, 128], BF16, "wdn")
    pgt = [P.ps([128, 512], F32, f"pg{i}") for i in range(2)]
    pvt = [P.ps([128, 512], F32, f"pv{i}") for i in range(2)]
    C.ps_main = pgt + pvt
    C.Bps_main = [Buf(f"psm{i}") for i in range(4)]
    C.pgring = Ring(P, 2, None, None, "pg", tensors=pgt, bufs=C.Bps_main[0:2])
    C.pvring = Ring(P, 2, None, None, "pv", tensors=pvt, bufs=C.Bps_main[2:4])
    C.poring = Ring(P, 4, None, None, "po", tensors=C.ps_main, bufs=C.Bps_main)
    pht = [P.ps([128, 512], F32, f"ph{i}") for i in range(2)]
    C.phring = Ring(P, 2, None, None, "ph", tensors=pht, bufs=[Buf("ph0"), Buf("ph1")])
    C.cring = Ring(P, 2, [128, TT], F32, "cv")
    C.hring = Ring(P, 2, [128, 2], F32, "hh")
    C.cw = P.sb([128, 3, NFT], F32, "cw")
    C.cb = P.sb([128, NFT], F32, "cb")
    C.Bcw = Buf("cw")


def load_conv(P, C, conv_w, conv_b):
    P.dma("sp", C.cw[:], conv_w, writes=[C.Bcw])
    P.dma("sp", C.cb[:], conv_b, writes=[C.Bcw])


POOL_WINDOWS = (2, 4, 8, 16)
HB = 18


def build_layer0(ntiles):
    nc = bass.Bass("TRN2", target_bir_lowering=False)
    NI = HB + TT
    dt = nc.dram_tensor
    xT_in = dt("xT_in", [ntiles, D, NI], F32, kind="ExternalInput").ap()
    valid = dt("valid", [ntiles, 128, HB], F32, kind="ExternalInput").ap()
    rc = dt("rc", [ntiles, 128, 4, HB], F32, kind="ExternalInput").ap()
    mod = dt("mod", [128, 6, NKC], F32, kind="ExternalInput").ap()
    ng = dt("ng", [128, 2, NKC], F32, kind="ExternalInput").ap()
    pool_w = dt("pool_w", [4, 1024, 1024], F32, kind="ExternalInput").ap()
    pool_scale = dt("pool_scale", [128, NKC], F32, kind="ExternalInput").ap()
    w_up = dt("w_up", [D, 2 * FF], F32, kind="ExternalInput").ap()
    conv_w = dt("conv_w", [128, 3, NFT], F32, kind="ExternalInput").ap()
    conv_b = dt("conv_b", [128, NFT], F32, kind="ExternalInput").ap()
    w_down = dt("w_down", [FF, D], F32, kind="ExternalInput").ap()
    x2T = dt("x2T", [ntiles, D, TT], F32, kind="ExternalOutput").ap()
    x1T = dt("x1T", [ntiles, D, 2 + TT], F32, kind="Internal").ap()

    P = Prog(nc)
    C = Ctx()
    setup_common(P, C)
    setup_ffn(P, C)
    load_mods(P, C, mod, ng)
    load_conv(P, C, conv_w, conv_b)
    C.Bconst = Buf("const")
    psc = P.sb([128, NKC], F32, "psc")
    gp = P.sb([128, NKC], F32, "gp")
    P.dma("sp", psc[:], pool_scale, writes=[C.Bconst])
    P.op("dve", lambda e: e.tensor_tensor(out=gp[:], in0=psc[:], in1=C.mod[:, 2, :], op=ALU.mult),
         [C.Bconst, C.Bmod], [C.Bconst])
    vt = P.sb([128, HB], F32, "valid_sb")
    rct = P.sb([128, 4, HB], F32, "rc_sb")
    Bvt = Buf("vt")
    pooled = C.actflat[:, 0:NKC * (2 + TT)].rearrange("p (k n) -> p k n", n=2 + TT)
    hbr = Ring(P, 2, [128, NI], F32, "hb")
    sar = Ring(P, 1, [128, NI], F32, "sa")
    sbr = Ring(P, 1, [128, NI], F32, "sbb")
    wpr = Ring(P, 2, [128, 8, 128], BF16, "wp")
    outs = []
    Bin = [Buf("xin") for _ in range(NKC)]
    for ti in range(ntiles):
        Bx1 = [Buf("x1T") for _ in range(NKC)]
        Bx2 = [Buf("x2T") for _ in range(NKC)]
        P.dma("sp", vt[:], valid[ti], writes=[Bvt])
        P.dma("sp", rct[:], rc[ti], writes=[Bvt])
        rms_stats(P, C, xT_in[ti], NI, Bin)
        cur = {}

        def sink(kc):
            hb, bhb = hbr.next()
            cur["hb"] = (hb, bhb)
            return hb[:, 0:NI], bhb

        def post(kc):
            hb, bhb = cur["hb"]
            P.op("dve", lambda e: e.tensor_tensor(out=hb[:, 0:HB], in0=hb[:, 0:HB], in1=vt[:, 0:HB], op=ALU.mult),
                 [bhb, Bvt], [bhb])
            wi = kc // 8
            s, bs = hb, bhb
            d = 1
            k = 0
            while d < POOL_WINDOWS[wi]:
                nt, bn = (sar if k % 2 == 0 else sbr).next()
                P.op("dve", lambda e, s=s, nt=nt, d=d: e.tensor_tensor(out=nt[:, d:NI], in0=s[:, d:NI],
                                                                      in1=s[:, 0:NI - d], op=ALU.add),
                     [bs], [bn])
                s, bs = nt, bn
                d *= 2
                k += 1
            P.op("dve", lambda e, s=s, wi=wi: e.tensor_tensor(out=s[:, 16:16 + HB], in0=s[:, 16:16 + HB],
                                                             in1=rct[:, wi, :], op=ALU.mult),
                 [bs, Bvt], [bs])
            P.op("dve", lambda e, s=s, hb=hb, kc=kc, wi=wi: e.scalar_tensor_tensor(
                out=pooled[:, kc, :], in0=s[:, 16:NI], scalar=1.0 / POOL_WINDOWS[wi], in1=hb[:, 16:NI],
                op0=ALU.mult, op1=ALU.subtract), [bs, bhb], [C.Bact[kc]])

        norm_apply(P, C, xT_in[ti], NI, Bin, C.A[:, 0, :], C.mod[:, 0, :], C.Bmod, sink, post)
        for i in range(NKC):
            g = i // 8
            wp, bwp = wpr.next()
            P.dma("pool", wp[:], pool_w[g, :, (i % 8) * 128:(i % 8 + 1) * 128].rearrange("(c p) e -> p c e", p=128),
                  writes=[bwp])
            po, bpo = C.poring.next()
            ph, bph = C.phring.next()
            for c in range(8):
                P.op("pe", lambda e, c=c, po=po, wp=wp, g=g: e.matmul(po[:, 0:TT], wp[:, c, :],
                                                                     pooled[:, 8 * g + c, 2:2 + TT],
                                                                     start=(c == 0), stop=(c == 7)),
                     [bwp, C.Bact[8 * g + c]], [bpo])
            for c in range(8):
                P.op("pe", lambda e, c=c, ph=ph, wp=wp, g=g: e.matmul(ph[:, 0:2], wp[:, c, :],
                                                                     pooled[:, 8 * g + c, 0:2],
                                                                     start=(c == 0), stop=(c == 7)),
                     [bwp, C.Bact[8 * g + c]], [bph])
            xt, bx = C.xring.next()
            P.dma("sp", xt[:, 0:2 + TT], xT_in[ti, i * 128:(i + 1) * 128, 16:NI], reads=[Bin[i]], writes=[bx])
            P.op("dve", lambda e, xt=xt, po=po, i=i: e.scalar_tensor_tensor(
                out=xt[:, 2:2 + TT], in0=po[:, 0:TT], scalar=gp[:, i:i + 1], in1=xt[:, 2:2 + TT], op0=ALU.mult,
                op1=ALU.add), [bpo, bx, C.Bconst], [bx])
            P.op("dve", lambda e, xt=xt, ph=ph, i=i: e.scalar_tensor_tensor(
                out=xt[:, 0:2], in0=ph[:, 0:2], scalar=gp[:, i:i + 1], in1=xt[:, 0:2], op0=ALU.mult,
                op1=ALU.add), [bph, bx, C.Bconst], [bx])
            P.dma("sp", x1T[ti, i * 128:(i + 1) * 128, :], xt[:, 0:2 + TT], reads=[bx], writes=[Bx1[i]])
        rms_stats(P, C, x1T[ti], 2 + TT, Bx1)
        norm_apply(P, C, x1T[ti], 2 + TT, Bx1, C.A[:, 1, :], C.mod[:, 3, :], C.Bmod,
                   lambda kc: (C.hT[:, kc, :], C.BhT))
        ffn_tile(P, C, 0, w_up, conv_w, conv_b, w_down, x1T[ti], Bx1, x2T[ti], Bx2, vt[:, 16:18])
        outs.extend(b.w for b in Bx2)
    P.finish(outs)
    P.emit()
    return nc


def fm(v):
    v = np.asarray(v, dtype=np.float32)
    lead = v.shape[:-1]
    r = v.reshape(lead + (NKC, 128))
    return np.ascontiguousarray(np.moveaxis(r, -1, 0))


def tile_xT(x_b, t0, halo, n):
    out = np.zeros((D, halo + n), np.float32)
    lo = max(0, t0 - halo)
    out[:, lo - (t0 - halo):] = x_b[lo:t0 + n].T
    return out


def layer0_consts(t0):
    valid = np.zeros((128, HB), np.float32)
    for j in range(HB):
        if t0 - HB + j >= 0:
            valid[:, j] = 1.0
    rc = np.ones((128, 4, HB), np.float32)
    for wi, w in enumerate(POOL_WINDOWS):
        for j in range(HB):
            t = t0 - 2 + j
            if t >= 0:
                rc[:, wi, j] = float(w) / min(t + 1, w)
    return valid, rc


def fm_ff(v):
    v = np.asarray(v, dtype=np.float32)
    lead = v.shape[:-1]
    r = v.reshape(lead + (NFT, 128))
    return np.ascontiguousarray(np.moveaxis(r, -1, 0))


MODC = 6 * D // NCORE


def build_mods():
    nc = bass.Bass("TRN2", target_bir_lowering=False)
    dt = nc.dram_tensor
    cT = dt("cT", [128, NKC, B], F32, kind="ExternalInput").ap()
    aw = dt("aw", [2, D, MODC], F32, kind="ExternalInput").ap()
    ab = dt("ab", [2, B, MODC], F32, kind="ExternalInput").ap()
    mo = dt("mo", [2, B, MODC], F32, kind="ExternalOutput").ap()
    P = Prog(nc)
    ct = P.sb([128, NKC, B], F32, "ct")
    Bct = Buf("ct")
    P.dma("sp", ct[:], cT, writes=[Bct])
    P.op("act", lambda e: e.activation(out=ct[:], in_=ct[:], func=AF.Silu), [Bct], [Bct])
    wr = Ring(P, 2, [128, NKC, 512], F32, "aw")
    pr = Ring(P, 2, None, None, "pm", tensors=[P.ps([128, 512], F32, f"pm{i}") for i in range(2)],
              bufs=[Buf("pm0"), Buf("pm1")])
    bt = P.sb([B, 2, MODC], F32, "abt")
    ot = P.sb([B, 2, MODC], F32, "mot")
    Bbt = Buf("bt")
    Bot = Buf("ot")
    for l in range(2):
        P.dma("sp", bt[:, l, :], ab[l], writes=[Bbt])
    outs = []
    for l in range(2):
        for nt in range(MODC // 512):
            w, bw = wr.next()
            P.dma("sp", w[:], aw[l, :, nt * 512:(nt + 1) * 512].rearrange("(kc p) n -> p kc n", p=128), writes=[bw])
            pm, bpm = pr.next()
            for kc in range(NKC):
                P.op("pe", lambda e, kc=kc, w=w, pm=pm: e.matmul(pm[0:B, :], ct[:, kc, :], w[:, kc, :],
                                                                start=(kc == 0), stop=(kc == NKC - 1)),
                     [bw, Bct], [bpm])
            P.op("dve", lambda e, pm=pm, l=l, nt=nt: e.tensor_tensor(
                out=ot[:, l, nt * 512:(nt + 1) * 512], in0=pm[0:B, :], in1=bt[:, l, nt * 512:(nt + 1) * 512],
                op=ALU.add), [bpm, Bbt], [Bot])
        outs.append(P.dma("sp", mo[l], ot[:, l, :], reads=[Bot]))
    P.finish(outs)
    P.emit()
    return nc


def build_layer1_tail(ntiles):
    nc = bass.Bass("TRN2", target_bir_lowering=False)
    NI = 2 + TT
    dt = nc.dram_tensor
    x1in = dt("x1in", [ntiles, D, NI], F32, kind="ExternalInput").ap()
    ygT = dt("ygT", [ntiles, D, NI], F32, kind="ExternalInput").ap()
    valid = dt("valid", [ntiles, 128, 2], F32, kind="ExternalInput").ap()
    mod = dt("mod", [128, 6, NKC], F32, kind="ExternalInput").ap()
    ng = dt("ng", [128, 2, NKC], F32, kind="ExternalInput").ap()
    fg = dt("fg", [128, NKC], F32, kind="ExternalInput").ap()
    wo = dt("wo", [D, D], F32, kind="ExternalInput").ap()
    w_up = dt("w_up", [D, 2 * FF], F32, kind="ExternalInput").ap()
    conv_w = dt("conv_w", [128, 3, NFT], F32, kind="ExternalInput").ap()
    conv_b = dt("conv_b", [128, NFT], F32, kind="ExternalInput").ap()
    w_down = dt("w_down", [FF, D], F32, kind="ExternalInput").ap()
    outT = dt("outT", [ntiles, D, TT], F32, kind="ExternalOutput").ap()
    xmT = dt("xmT", [ntiles, D, NI], F32, kind="Internal").ap()
    x3T = dt("x3T", [ntiles, D, TT], F32, kind="Internal").ap()

    P = Prog(nc)
    C = Ctx()
    setup_common(P, C)
    setup_ffn(P, C)
    load_mods(P, C, mod, ng)
    load_conv(P, C, conv_w, conv_b)
    C.Bconst = Buf("const")
    fgt = P.sb([128, NKC], F32, "fg_sb")
    P.dma("sp", fgt[:], fg, writes=[C.Bconst])
    vt = P.sb([128, 2], F32, "valid_sb")
    ygb = C.actflat[:, 0:NKC * NI].rearrange("p (k n) -> p k n", n=NI)
    oring = Ring(P, 2, [128, TT], F32, "ot")
    Bin = [Buf("xin") for _ in range(NKC)]
    Byg = Buf("ygdram")
    outs = []
    for ti in range(ntiles):
        Bxm = [Buf("xmT") for _ in range(NKC)]
        Bx3 = [Buf("x3T") for _ in range(NKC)]
        Bo = [Buf("outT") for _ in range(NKC)]
        P.dma("sp", vt[:], valid[ti], writes=[C.Bconst])
        for kc in range(NKC):
            P.dma("pool", ygb[:, kc, :], ygT[ti, kc * 128:(kc + 1) * 128, :], reads=[Byg], writes=[C.Bact[kc]])
        for i in range(NKC):
            ws, bws = C.wuring.next()
            P.dma("pool", ws[:], wo[:, i * 128:(i + 1) * 128].rearrange("(kc p) c -> p kc c", p=128), writes=[bws])
            po, bpo = C.poring.next()
            ph, bph = C.phring.next()
            for kc in range(NKC):
                P.op("pe", lambda e, kc=kc, po=po, ws=ws: e.matmul(po[:, 0:TT], ws[:, kc, :], ygb[:, kc, 2:NI],
                                                                  start=(kc == 0), stop=(kc == NKC - 1)),
                     [bws, C.Bact[kc]], [bpo])
            for kc in range(NKC):
                P.op("pe", lambda e, kc=kc, ph=ph, ws=ws: e.matmul(ph[:, 0:2], ws[:, kc, :], ygb[:, kc, 0:2],
                                                                  start=(kc == 0), stop=(kc == NKC - 1)),
                     [bws, C.Bact[kc]], [bph])
            xt, bx = C.xring.next()
            P.dma("sp", xt[:, 0:NI], x1in[ti, i * 128:(i + 1) * 128, :], reads=[Bin[i]], writes=[bx])
            P.op("dve", lambda e, xt=xt, po=po, i=i: e.scalar_tensor_tensor(
                out=xt[:, 2:NI], in0=po[:, 0:TT], scalar=C.mod[:, 2, i:i + 1], in1=xt[:, 2:NI], op0=ALU.mult,
                op1=ALU.add), [bpo, bx, C.Bmod], [bx])
            P.op("dve", lambda e, xt=xt, ph=ph, i=i: e.scalar_tensor_tensor(
                out=xt[:, 0:2], in0=ph[:, 0:2], scalar=C.mod[:, 2, i:i + 1], in1=xt[:, 0:2], op0=ALU.mult,
                op1=ALU.add), [bph, bx, C.Bmod], [bx])
            P.dma("sp", xmT[ti, i * 128:(i + 1) * 128, :], xt[:, 0:NI], reads=[bx], writes=[Bxm[i]])
        rms_stats(P, C, xmT[ti], NI, Bxm)
        norm_apply(P, C, xmT[ti], NI, Bxm, C.A[:, 1, :], C.mod[:, 3, :], C.Bmod, lambda kc: (C.hT[:, kc, :], C.BhT))
        ffn_tile(P, C, 1, w_up, conv_w, conv_b, w_down, xmT[ti], Bxm, x3T[ti], Bx3, vt[:, 0:2])
        rms_stats(P, C, x3T[ti], TT, Bx3)
        cur = {}

        def sink(kc):
            o, bo = oring.next()
            cur["o"] = (o, bo)
            return o[:, 0:TT], bo

        def post(kc, ti=ti):
            o, bo = cur["o"]
            outs.append(P.dma("sp", outT[ti, kc * 128:(kc + 1) * 128, :], o[:, 0:TT], reads=[bo], writes=[Bo[kc]]))

        norm_apply(P, C, x3T[ti], TT, Bx3, fgt, None, C.Bconst, sink, post)
    P.finish(outs)
    P.emit()
    return nc


SEG = 256
CH = 64
NCH = SEG // CH
NHP = 8
HGF = 1024
GL = 480
EXPM05 = 0.6065306597126334


class PsumMgr:
    def __init__(self, P, nbanks, name="pb"):
        self.t = [P.ps([128, 512], F32, f"{name}{i}") for i in range(nbanks)]
        self.b = [Buf(f"{name}{i}") for i in range(nbanks)]
        self.n = nbanks
        self.bank = 0

    def alloc(self, ncols):
        k = self.bank
        self.bank = (self.bank + 1) % self.n
        return self.t[k], 0, [self.b[k]]


def build_rwkv(nseg=T // SEG):
    nc = bass.Bass("TRN2", target_bir_lowering=False)
    dt = nc.dram_tensor
    NT = nseg * SEG
    xT = dt("xT", [D, NT], F32, kind="ExternalInput").ap()
    mod = dt("mod", [128, 6, NKC], F32, kind="ExternalInput").ap()
    ng = dt("ng", [128, 2, NKC], F32, kind="ExternalInput").ap()
    mu = dt("mu", [128, 6, NKC], F32, kind="ExternalInput").ap()
    wr = dt("wr", [D, HGF], F32, kind="ExternalInput").ap()
    wk = dt("wk", [D, HGF], F32, kind="ExternalInput").ap()
    wv = dt("wv", [D, HGF], F32, kind="ExternalInput").ap()
    w1 = dt("w1", [D, 128], F32, kind="ExternalInput").ap()
    w2 = dt("w2", [128, HGF], F32, kind="ExternalInput").ap()
    a1 = dt("a1", [D, 128], F32, kind="ExternalInput").ap()
    a2 = dt("a2", [128, HGF], F32, kind="ExternalInput").ap()
    g1 = dt("g1", [D, 512], F32, kind="ExternalInput").ap()
    g2 = dt("g2", [512, HGF], F32, kind="ExternalInput").ap()
    vecs = dt("vecs", [128, 7, NHP], F32, kind="ExternalInput").ap()
    cm = dt("cm", [128, 9, 128], F32, kind="ExternalInput").ap()
    rmask = dt("rmask", [128, SEG], F32, kind="ExternalInput").ap()
    ygT = dt("ygT", [HGF, NT], F32, kind="ExternalOutput").ap()

    P = Prog(nc)
    C = Ctx()
    PM = PsumMgr(P, 7)
    ssb = P.ps([128, 512], F32, "ssbank")
    setup_common(P, C, W=SEG + 4, ps_ss=([ssb], [Buf("ssbank")]))
    load_mods(P, C, mod, ng)
    Bc = Buf("consts")
    mut = P.sb([128, 6, NKC], F32, "mu_sb")
    omu = P.sb([128, 6, NKC], F32, "omu_sb")
    vt = P.sb([128, 7, NHP], F32, "vecs_sb")
    cmt = P.sb([128, 9, 128], F32, "cm_sb")
    rmt = P.sb([128, SEG], F32, "rmask_sb")
    P.dma("sp", mut[:], mu, writes=[Bc])
    P.dma("sp", vt[:], vecs, writes=[Bc])
    P.dma("sp", cmt[:], cm, writes=[Bc])
    P.dma("sp", rmt[:], rmask, writes=[Bc])
    P.op("dve", lambda e: e.tensor_scalar(out=omu[:], in0=mut[:], scalar1=-1.0, scalar2=1.0, op0=ALU.mult,
                                          op1=ALU.add), [Bc], [Bc])
    ident = cmt[:, 0, :]
    bones = cmt[:, 1, :]
    mask2 = cmt[:, 2:4, :]
    I2 = cmt[:, 4, 0:64]
    ml4 = cmt[:, 5:9, :]
    eps_l = P.sb([128, 1], F32, "eps_l")
    P.op("dve", lambda e: e.memset(eps_l[:], LNX_EPS), (), [Bc])
    w2t = P.sb([128, HGF], BF16, "w2_sb")
    a2t = P.sb([128, HGF], BF16, "a2_sb")
    g2t = P.sb([128, 4, HGF], BF16, "g2_sb")
    P.dma("pool", w2t[:], w2, writes=[Bc])
    P.dma("pool", a2t[:], a2, writes=[Bc])
    P.dma("pool", g2t[:], g2.rearrange("(c p) n -> p c n", p=128), writes=[Bc])
    hbuf = P.sb([128, NKC, SEG + 1], F32, "hbuf")
    Bh = Buf("hbuf")
    P.op("dve", lambda e: e.memset(hbuf[:, :, SEG:SEG + 1], 0.0), (), [Bh])
    mixb = P.sb([128, NKC, SEG], BF16, "mixb")
    Bmix = Buf("mix")
    mtr = Ring(P, 2, [128, SEG], F32, "mt")
    wring = Ring(P, 2, [128, NKC, 128], BF16, "wsl")
    FS = {}
    BFS = {}
    for nm in ("r", "k", "v", "s", "a", "bv"):
        FS[nm] = P.sb([128, NHP, SEG], F32, f"fs_{nm}")
        BFS[nm] = [Buf(f"fs_{nm}{o}") for o in range(NHP)]
    BRc = [[Buf(f"fs_r{o}_{c}") for c in range(NCH)] for o in range(NHP)]
    FS["yo"] = FS["r"]
    GC = P.sb([128, NHP, NCH], F32, "GC")
    BGC = Buf("GC")
    lora = P.sb([128, 4, SEG], BF16, "lora")
    Blora = Buf("lora")
    tmp_slots = {"A": (C.xring.t[0], C.xring.b[0]), "B": (C.xring.t[1], C.xring.b[1]),
                 "C": (C.xring.t[2], C.xring.b[2]), "D": (C.sqring.t[0], C.sqring.b[0]),
                 "E": (C.sqring.t[1], C.sqring.b[1]), "G": (C.tring.t[0], C.tring.b[0])}
    AR = P.sb([128, NHP, 256], F32, "AR")
    Bt = P.sb([128, NHP, 128], F32, "Btf")
    Kt = P.sb([128, NHP, 128], F32, "Ktf")
    Vt = P.sb([128, NHP, 128], F32, "Vtf")
    YB = Vt
    BFB = {n: [Buf(f"fb_{n}{o}") for o in range(NHP)] for n in ("AR", "Bt", "Kt", "Vt")}
    BFB["YB"] = BFB["Vt"]
    for tns, n in ((AR, "AR"), (Bt, "Bt"), (Kt, "Kt"), (Vt, "Vt")):
        P.op("pool", lambda e, tns=tns: e.memset(tns[:], 0.0), (), BFB[n])
    Zs = [P.sb([128, NHP, 64], F32, f"Z{i}") for i in range(2)]
    BZ = [[Buf(f"Z{i}_{o}") for o in range(NHP)] for i in range(2)]
    P.op("pool", lambda e: e.memset(Zs[0][:], 0.0), (), BZ[0])
    LM = P.sb([128, NHP, 2, 256], F32, "LM")
    BLM = [[Buf(f"LM{o}_{i}") for i in range(2)] for o in range(NHP)]
    QQ = [P.sb([128, NHP, 128], F32, f"QQ{i}") for i in range(2)]
    BQQ = [[Buf(f"QQ{i}_{o}") for o in range(NHP)] for i in range(2)]
    Xs = [P.sb([128, NHP, 64], F32, f"X{i}") for i in range(2)]
    BX = [Buf("X0"), Buf("X1")]
    Vts = P.sb([128, NHP, 64], F32, "Vts")
    BVts = Buf("Vts")
    BKT = QQ
    BBKT = BQQ
    st = {n: P.sb([128, NHP], F32, f"st_{n}") for n in ("s1", "s2", "mean", "msq", "var")}
    Bst = Buf("stats")
    sqy = P.sb([128, NHP, 64], F32, "sqy")
    yn = P.sb([128, NHP, 64], F32, "yn")
    Byn = Buf("yn")
    GLb = vt

    Bxin = [Buf("xin") for _ in range(NKC)]
    outs = []
    zi = 0
    def do_seg(sg, zi):
        c0 = sg * SEG
        P.op("act", lambda e: e.activation(out=hbuf[:, :, 0:1], in_=hbuf[:, :, SEG:SEG + 1], func=AF.Identity,
                                           scale=1.0, bias=0.0), [Bh], [Bh])
        rms_stats(P, C, xT[:, c0:c0 + SEG], SEG, Bxin)
        norm_apply(P, C, xT[:, c0:c0 + SEG], SEG, Bxin, C.A[:, 0, :], C.mod[:, 0, :], C.Bmod,
                   lambda kc: (hbuf[:, kc, 1:SEG + 1], Bh))

        def make_mix(n):
            for kc in range(NKC):
                mt, bmt = mtr.next()
                P.op("act", lambda e, mt=mt, kc=kc: e.activation(
                    out=mt[:], in_=hbuf[:, kc, 0:SEG], func=AF.Identity, scale=mut[:, n, kc:kc + 1], bias=0.0),
                    [Bh, Bc], [bmt])
                P.op("dve", lambda e, mt=mt, kc=kc: e.scalar_tensor_tensor(
                    out=mixb[:, kc, :], in0=hbuf[:, kc, 1:SEG + 1], scalar=omu[:, n, kc:kc + 1], in1=mt[:],
                    op0=ALU.mult, op1=ALU.add), [Bh, Bc, bmt], [Bmix])

        def proj_full(w_ap, ncols, evac):
            for ot in range(ncols // 128):
                ws, bws = wring.next()
                P.dma("pool", ws[:], w_ap[:, ot * 128:(ot + 1) * 128].rearrange("(kc p) c -> p kc c", p=128),
                      writes=[bws])
                pt, pc, pb = PM.alloc(SEG)
                for kc in range(NKC):
                    P.op("pe", lambda e, kc=kc, ws=ws, pt=pt, pc=pc: e.matmul(
                        pt[:, pc:pc + SEG], ws[:, kc, :], mixb[:, kc, :], start=(kc == 0), stop=(kc == NKC - 1)),
                        [bws, Bmix], pb)
                evac(ot, pt, pc, pb)

        def evac_copy(nm):
            def f(ot, pt, pc, pb):
                P.op("act", lambda e: e.activation(out=FS[nm][:, ot, :], in_=pt[:, pc:pc + SEG], func=AF.Identity,
                                                   scale=1.0, bias=0.0), pb, BRc[ot] if nm == "r" else [BFS[nm][ot]])
            return f

        for n, nm, w_ap in ((0, "r", wr), (2, "k", wk), (3, "v", wv)):
            make_mix(n)
            proj_full(w_ap, HGF, evac_copy(nm))
        make_mix(1)
        proj_full(w1, 128, lambda ot, pt, pc, pb: P.op("act", lambda e: e.activation(
            out=lora[:, 0, :], in_=pt[:, pc:pc + SEG], func=AF.Tanh), pb, [Blora]))
        for ot in range(NHP):
            pt, pc, pb = PM.alloc(SEG)
            P.op("pe", lambda e, ot=ot, pt=pt, pc=pc: e.matmul(pt[:, pc:pc + SEG], w2t[:, ot * 128:(ot + 1) * 128],
                                                             lora[:, 0, :], start=True, stop=True), [Bc, Blora], pb)
            P.op("act", lambda e, ot=ot, pt=pt, pc=pc: e.activation(
                out=FS["s"][:, ot, :], in_=pt[:, pc:pc + SEG], func=AF.Sigmoid, bias=vt[:, 0, ot:ot + 1], scale=1.0),
                pb + [Bc], [BFS["s"][ot]])
        make_mix(4)
        proj_full(a1, 128, lambda ot, pt, pc, pb: P.op("act", lambda e: e.activation(
            out=lora[:, 0, :], in_=pt[:, pc:pc + SEG], func=AF.Identity, scale=1.0, bias=0.0), pb, [Blora]))
        for ot in range(NHP):
            pt, pc, pb = PM.alloc(SEG)
            P.op("pe", lambda e, ot=ot, pt=pt, pc=pc: e.matmul(pt[:, pc:pc + SEG], a2t[:, ot * 128:(ot + 1) * 128],
                                                             lora[:, 0, :], start=True, stop=True), [Bc, Blora], pb)
            P.op("act", lambda e, ot=ot, pt=pt, pc=pc: e.activation(
                out=FS["a"][:, ot, :], in_=pt[:, pc:pc + SEG], func=AF.Sigmoid, bias=vt[:, 1, ot:ot + 1], scale=1.0),
                pb + [Bc], [BFS["a"][ot]])

        for o in range(NHP):
            R_, K_, V_, S_, A_, BV_ = (FS[n][:, o, :] for n in ("r", "k", "v", "s", "a", "bv"))
            bK, bV, bS, bA, bBV = (BFS[n][o] for n in ("k", "v", "s", "a", "bv"))
            (tA, bA_), (tB, bB_), (tC, bC_), (tD, bD_), (tE, bE_), (tG, bG_) = (
                ((tmp_slots[n][0][:, 0:SEG], tmp_slots[n][1])) for n in "ABCDEG")
            P.op("dve", lambda e, tA=tA, K_=K_, o=o: e.tensor_scalar(out=tA[:], in0=K_, scalar1=vt[:, 2, o:o + 1],
                                                                   scalar2=None, op0=ALU.mult), [bK, Bc], [bA_])
            P.op("act", lambda e, tA=tA, tB=tB: e.activation(out=tB[:], in_=tA[:], func=AF.Square), [bA_], [bB_])
            pt, pc, pb = PM.alloc(SEG)
            P.op("pe", lambda e, tB=tB, pt=pt, pc=pc: e.matmul(pt[:, pc:pc + SEG], bones, tB[:], start=True, stop=True),
                 [bB_, Bc], pb)
            P.op("act", lambda e, tB=tB, pt=pt, pc=pc: e.activation(out=tB[:], in_=pt[:, pc:pc + SEG], func=AF.Sqrt),
                 pb + [bB_], [bB_])
            P.op("dve", lambda e, tB=tB: e.tensor_scalar(out=tB[:], in0=tB[:], scalar1=1e-12, scalar2=None,
                                                        op0=ALU.max), [bB_], [bB_])
            P.op("dve", lambda e, tB=tB: e.reciprocal(out=tB[:], in_=tB[:]), [bB_], [bB_])
            P.op("dve", lambda e, tA=tA, tB=tB: e.tensor_tensor(out=tA[:], in0=tA[:], in1=tB[:], op=ALU.mult),
                 [bA_, bB_], [bA_])
            P.op("dve", lambda e, tC=tC, A_=A_, o=o: e.tensor_scalar(
                out=tC[:], in0=A_, scalar1=-1.0, scalar2=vt[:, 3, o:o + 1], op0=ALU.add, op1=ALU.mult),
                [bA, Bc], [bC_])
            P.op("dve", lambda e, tC=tC, K_=K_: e.scalar_tensor_tensor(out=K_, in0=tC[:], scalar=1.0, in1=K_,
                                                                      op0=ALU.add, op1=ALU.mult), [bC_, bK], [bK])
            P.op("dve", lambda e, tC=tC, R_=R_, K_=K_, o=o: e.scalar_tensor_tensor(
                out=tC[:], in0=R_, scalar=vt[:, 4, o:o + 1], in1=K_, op0=ALU.mult, op1=ALU.mult),
                BRc[o] + [bK, Bc, bC_], [bC_])
            pt, pc, pb = PM.alloc(SEG)
            P.op("pe", lambda e, tC=tC, pt=pt, pc=pc: e.matmul(pt[:, pc:pc + SEG], bones, tC[:], start=True, stop=True),
                 [bC_, Bc], pb)
            P.op("dve", lambda e, V_=V_, BV_=BV_, pt=pt, pc=pc: e.tensor_tensor(out=BV_, in0=V_, in1=pt[:, pc:pc + SEG],
                                                                               op=ALU.mult), pb + [bV], [bBV])
            P.op("act", lambda e, tC=tC, S_=S_: e.mul(out=tC[:], in_=S_, mul=-EXPM05), [bS, bC_], [bC_])
            P.op("dve", lambda e, tC=tC, tD=tD: e.tensor_tensor_scan(out=tD[:], data0=rmt[:], data1=tC[:], initial=0.0,
                                                                    op0=ALU.mult, op1=ALU.add), [bC_, Bc], [bD_])
            P.op("dve", lambda e, tC=tC, tD=tD, tE=tE: e.tensor_tensor(out=tE[:], in0=tD[:], in1=tC[:],
                                                                      op=ALU.subtract), [bC_, bD_], [bE_])
            P.op("act", lambda e, tD=tD, tG=tG: e.activation(out=tG[:], in_=tD[:], func=AF.Exp), [bD_], [bG_])
            P.op("act", lambda e, tD=tD: e.activation(out=tD[:], in_=tD[:], func=AF.Exp, scale=-1.0), [bD_], [bD_])
            P.op("act", lambda e, tE=tE: e.activation(out=tE[:], in_=tE[:], func=AF.Exp), [bE_], [bE_])
            P.op("act", lambda e, tG=tG, o=o: e.activation(
                out=GC[:, o, :], in_=tG[:, CH - 1:SEG:CH], func=AF.Identity, scale=1.0, bias=0.0), [bG_], [BGC])
            P.op("dve", lambda e, R_=R_, tG=tG: e.tensor_tensor(out=R_, in0=R_, in1=tG[:], op=ALU.mult),
                 BRc[o] + [bG_], BRc[o])
            P.op("dve", lambda e, K_=K_, tD=tD: e.tensor_tensor(out=K_, in0=K_, in1=tD[:], op=ALU.mult),
                 [bK, bD_], [bK])
            P.op("dve", lambda e, S_=S_, tA=tA, A_=A_: e.tensor_tensor(out=S_, in0=tA[:], in1=A_, op=ALU.mult),
                 [bA_, bA, bS], [bS])
            P.op("dve", lambda e, S_=S_, tD=tD: e.tensor_tensor(out=S_, in0=S_, in1=tD[:], op=ALU.mult),
                 [bS, bD_], [bS])
            P.op("dve", lambda e, A_=A_, tA=tA, tE=tE: e.scalar_tensor_tensor(
                out=A_, in0=tA[:], scalar=-1.0, in1=tE[:], op0=ALU.mult, op1=ALU.mult), [bA_, bE_, bA], [bA])

        def do_chunk(c, zi):
            cs = slice(c * CH, (c + 1) * CH)
            Z0, BZ0 = Zs[zi], BZ[zi]
            Z1, BZ1 = Zs[1 - zi], BZ[1 - zi]
            zi = 1 - zi
            fb_src = ((AR, 0, "a", "AR"), (AR, 128, "r", "AR"), (Bt, 0, "s", "Bt"), (Kt, 0, "k", "Kt"),
                      (Vt, 0, "v", "Vt"))
            for qi, (dst, off, nm, bn) in enumerate(fb_src):
                for h in range(2):
                    eng = "pool" if (qi + h) % 2 == 0 else "act"
                    ps_ = slice(64 * h, 64 * h + 64)
                    srcb = [BRc[o][c] for o in range(NHP)] if nm == "r" else BFS[nm]
                    if eng == "pool":
                        P.op("pool", lambda e, dst=dst, off=off, nm=nm, h=h, ps_=ps_: e.tensor_copy(
                            out=dst[ps_, :, off + 64 * h:off + 64 * h + 64], in_=FS[nm][ps_, :, cs]),
                            srcb, BFB[bn])
                    else:
                        P.op("act", lambda e, dst=dst, off=off, nm=nm, h=h, ps_=ps_: e.activation(
                            out=dst[ps_, :, off + 64 * h:off + 64 * h + 64], in_=FS[nm][ps_, :, cs], func=AF.Identity,
                            scale=1.0, bias=0.0), srcb, BFB[bn])
            for g4 in range(2):
                pt, pc, pb = PM.alloc(512)
                for q in range(4):
                    o = g4 * 4 + q
                    P.op("pe", lambda e, o=o, q=q, pt=pt: e.transpose(pt[:, q * 128:(q + 1) * 128], Vt[:, o, :], ident),
                         [BFB["Vt"][o], Bc], pb)
                for h in range(2):
                    ps_ = slice(64 * h, 64 * h + 64)
                    src = pt[ps_, :].rearrange("p (q n) -> p q n", n=128)[:, :, 64 * h:64 * h + 64]
                    P.op("dve", lambda e, src=src, g4=g4, ps_=ps_: e.tensor_copy(out=Vts[ps_, g4 * 4:g4 * 4 + 4, :],
                                                                               in_=src), pb, [BVts])
            for o in range(NHP):
                for wi, (lt, bn) in enumerate(((Bt, "Bt"), (Kt, "Kt"))):
                    pt, pc, pb = PM.alloc(256)
                    P.op("pe", lambda e, o=o, lt=lt, pt=pt, pc=pc: e.matmul(pt[:, pc:pc + 256], lt[:, o, :], AR[:, o, :],
                                                                          start=True, stop=True),
                         [BFB[bn][o], BFB["AR"][o]], pb)
                    P.op("dve", lambda e, o=o, wi=wi, pt=pt, pc=pc: e.tensor_tensor(
                        out=LM[:, o, wi, :], in0=pt[:, pc:pc + 256], in1=cmt[:, 2:4, :], op=ALU.mult),
                        pb + [Bc], [BLM[o][wi]])
            for g4 in range(2):
                pt, pc, pb = PM.alloc(512)
                for q in range(4):
                    o = g4 * 4 + q
                    P.op("pe", lambda e, o=o, q=q, pt=pt: e.matmul(pt[:, q * 128:(q + 1) * 128], AR[:, o, 0:128],
                                                                  Bt[:, o, :], start=True, stop=True),
                         [BFB["AR"][o], BFB["Bt"][o]], pb)
                P.op("dve", lambda e, pt=pt, g4=g4: e.tensor_tensor(
                    out=QQ[0][:, g4 * 4:g4 * 4 + 4, :], in0=pt[:, :].rearrange("p (q n) -> p q n", n=128),
                    in1=ml4[:], op=ALU.mult), pb + [Bc],
                    BQQ[0][g4 * 4:g4 * 4 + 4])
            pt, pc, pb = PM.alloc(512)
            for o in range(NHP):
                P.op("pe", lambda e, o=o, pt=pt: e.matmul(pt[:, o * 64:(o + 1) * 64], AR[:, o, 0:128], Z0[:, o, :],
                                                         start=True, stop=False), [BFB["AR"][o], BZ0[o]], pb)
                P.op("pe", lambda e, o=o, pt=pt: e.matmul(pt[:, o * 64:(o + 1) * 64], LM[:, o, 1, 0:128], Vts[:, o, :],
                                                         start=False, stop=True), [BLM[o][1], BVts], pb)
            xi = 0
            P.op("act", lambda e, pt=pt: e.activation(out=Xs[0][:], in_=pt[:, :].rearrange("p (o n) -> p o n", n=64),
                                                     func=AF.Identity, scale=1.0, bias=0.0), pb, [BX[0]])
            for lv in range(6):
                pi = lv % 2

                def Pl(o, pi=pi):
                    return LM[:, o, pi, 0:128]

                def BPl(o, pi=pi):
                    return BLM[o][pi]
                qcur = lv % 2
                pt, pc, pb = PM.alloc(512)
                for o in range(NHP):
                    P.op("pe", lambda e, o=o, pt=pt, lhs=Pl(o), xi=xi: e.matmul(pt[:, o * 64:(o + 1) * 64], lhs,
                                                                              Xs[xi][:, o, :], start=True, stop=True),
                         [BPl(o), BX[xi]], pb)
                P.op("dve", lambda e, pt=pt, xi=xi: e.tensor_tensor(
                    out=Xs[1 - xi][:], in0=pt[:, :].rearrange("p (o n) -> p o n", n=64), in1=Xs[xi][:], op=ALU.add),
                    pb + [BX[xi]], [BX[1 - xi]])
                xi = 1 - xi
                if lv < 5:
                    npi = 1 - pi
                    pnew = []
                    for g4 in range(2):
                        pt, pc, pb = PM.alloc(512)
                        for q in range(4):
                            o = g4 * 4 + q
                            P.op("pe", lambda e, o=o, q=q, pt=pt, rhs=Pl(o), qcur=qcur: e.matmul(
                                pt[:, q * 128:(q + 1) * 128], QQ[qcur][:, o, :], rhs, start=True, stop=True),
                                [BQQ[qcur][o], BPl(o)], pb)
                        pnew.append((pt, pb, g4))
                    if lv < 4:
                        for g4 in range(2):
                            pt, pc, pb = PM.alloc(512)
                            for q in range(4):
                                o = g4 * 4 + q
                                P.op("pe", lambda e, o=o, q=q, pt=pt, lhs=Pl(o), qcur=qcur: e.matmul(
                                    pt[:, q * 128:(q + 1) * 128], lhs, QQ[qcur][:, o, :], start=True, stop=True),
                                    [BQQ[qcur][o], BPl(o)], pb)
                            P.op("dve", lambda e, pt=pt, g4=g4, qcur=qcur: e.tensor_copy(
                                out=QQ[1 - qcur][:, g4 * 4:g4 * 4 + 4, :],
                                in_=pt[:, :].rearrange("p (q n) -> p q n", n=128)), pb,
                                BQQ[1 - qcur][g4 * 4:g4 * 4 + 4])
                    for (pt, pb, g4) in pnew:
                        P.op("act", lambda e, pt=pt, g4=g4, npi=npi: e.activation(
                            out=LM[:, g4 * 4:g4 * 4 + 4, npi, 0:128],
                            in_=pt[:, :].rearrange("p (q n) -> p q n", n=128),
                            func=AF.Identity, scale=1.0, bias=0.0), pb, [BLM[o][npi] for o in range(g4 * 4, g4 * 4 + 4)])
            U = Xs[xi]
            BU = BX[xi]
            for wi, (src_t, bn) in enumerate(((Bt, "Bt"), (Kt, "Kt"))):
                for g4 in range(2):
                    pt, pc, pb = PM.alloc(512)
                    for q in range(4):
                        o = g4 * 4 + q
                        P.op("pe", lambda e, o=o, q=q, pt=pt, src_t=src_t: e.transpose(
                            pt[:, q * 128:(q + 1) * 128], src_t[:, o, :], ident), [BFB[bn][o], Bc], pb)
                    P.op("act", lambda e, pt=pt, wi=wi, g4=g4: e.activation(
                        out=BKT[wi][:, g4 * 4:g4 * 4 + 4, :], in_=pt[:, :].rearrange("p (q n) -> p q n", n=128),
                        func=AF.Identity, scale=1.0, bias=0.0), pb, BBKT[wi][g4 * 4:g4 * 4 + 4])

            pty, pc, pby = PM.alloc(512)
            for o in range(NHP):
                P.op("pe", lambda e, o=o: e.matmul(pty[:, o * 64:(o + 1) * 64], AR[:, o, 128:256], Z0[:, o, :],
                                                  start=True, stop=False), [BFB["AR"][o], BZ0[o]], pby)
                P.op("pe", lambda e, o=o, U=U: e.matmul(pty[:, o * 64:(o + 1) * 64], LM[:, o, 0, 128:256], U[:, o, :],
                                                       start=False, stop=False), [BLM[o][0], BU], pby)
                P.op("pe", lambda e, o=o: e.matmul(pty[:, o * 64:(o + 1) * 64], LM[:, o, 1, 128:256], Vts[:, o, :],
                                                  start=False, stop=True), [BLM[o][1], BVts], pby)
            ptz, pc, pbz = PM.alloc(512)
            for o in range(NHP):
                P.op("pe", lambda e, o=o: e.matmul(ptz[:, o * 64:(o + 1) * 64], ident, Z0[:, o, :], start=True,
                                                  stop=False), [Bc, BZ0[o]], pbz)
                P.op("pe", lambda e, o=o, U=U: e.matmul(ptz[:, o * 64:(o + 1) * 64], BKT[0][:, o, :], U[:, o, :],
                                                       start=False, stop=False), [BBKT[0][o], BU], pbz)
                P.op("pe", lambda e, o=o: e.matmul(ptz[:, o * 64:(o + 1) * 64], BKT[1][:, o, :], Vts[:, o, :],
                                                  start=False, stop=True), [BBKT[1][o], BVts], pbz)
            for o in range(NHP):
                P.op("act", lambda e, o=o, c=c: e.activation(out=Z1[:, o, :], in_=ptz[:, o * 64:(o + 1) * 64],
                                                            func=AF.Identity, scale=GC[:, o, c:c + 1], bias=0.0),
                     pbz + [BGC], [BZ1[o]])
            y3 = pty[:, :].rearrange("p (o n) -> p o n", n=64)
            P.op("dve", lambda e, y3=y3: e.tensor_reduce(out=st["s1"][:], in_=y3, axis=AX.X, op=ALU.add), pby, [Bst])
            P.op("act", lambda e, y3=y3: e.activation(out=sqy[:], in_=y3, func=AF.Square), pby, [Byn])
            P.op("dve", lambda e: e.tensor_reduce(out=st["s2"][:], in_=sqy[:], axis=AX.X, op=ALU.add), [Byn], [Bst])
            P.op("dve", lambda e: e.tensor_scalar(out=st["mean"][:], in0=st["s1"][:], scalar1=1.0 / 64, scalar2=None,
                                                  op0=ALU.mult), [Bst], [Bst])
            P.op("dve", lambda e: e.tensor_tensor(out=st["msq"][:], in0=st["mean"][:], in1=st["mean"][:],
                                                  op=ALU.mult), [Bst], [Bst])
            P.op("dve", lambda e: e.scalar_tensor_tensor(out=st["var"][:], in0=st["s2"][:], scalar=1.0 / 64,
                                                         in1=st["msq"][:], op0=ALU.mult, op1=ALU.subtract),
                 [Bst], [Bst])
            P.op("act", lambda e: e.activation(out=st["var"][:], in_=st["var"][:], func=AF.Sqrt, bias=eps_l[:],
                                               scale=1.0), [Bst, Bc], [Bst])
            P.op("dve", lambda e: e.reciprocal(out=st["var"][:], in_=st["var"][:]), [Bst], [Bst])
            for o in range(NHP):
                P.op("dve", lambda e, o=o: e.tensor_scalar(
                    out=yn[:, o, :], in0=pty[:, o * 64:(o + 1) * 64], scalar1=st["mean"][:, o:o + 1],
                    scalar2=st["var"][:, o:o + 1], op0=ALU.subtract, op1=ALU.mult), pby + [Bst], [Byn])
            for h in range(2):
                ps_ = slice(64 * h, 64 * h + 64)
                P.op("pool", lambda e, h=h, ps_=ps_: e.tensor_copy(out=YB[ps_, :, 64 * h:64 * h + 64], in_=yn[ps_, :, :]),
                     [Byn], BFB["YB"])
            pto, pc, pbo = PM.alloc(512)
            for o in range(NHP):
                P.op("pe", lambda e, o=o: e.matmul(pto[:, o * 64:(o + 1) * 64], YB[:, o, :], I2, start=True, stop=True),
                     [BFB["YB"][o], Bc], pbo)
            for o in range(NHP):
                P.op("act", lambda e, o=o, cs=cs: e.activation(
                    out=FS["yo"][:, o, cs], in_=pto[:, o * 64:(o + 1) * 64], func=AF.Identity,
                    scale=vt[:, 5, o:o + 1], bias=vt[:, 6, o:o + 1]), pbo + [Bc], [BRc[o][c]])
            return zi

        for c in range(NCH):
            zi = do_chunk(c, zi)
        make_mix(5)
        proj_full(g1, 512, lambda ot, pt, pc, pb: P.op("act", lambda e: e.activation(
            out=lora[:, ot, :], in_=pt[:, pc:pc + SEG], func=AF.Sigmoid), pb, [Blora]))
        for o in range(NHP):
            P.op("dve", lambda e, o=o: e.tensor_tensor(out=FS["yo"][:, o, :], in0=FS["yo"][:, o, :],
                                                      in1=FS["bv"][:, o, :], op=ALU.add),
                 BRc[o] + [BFS["bv"][o]], BRc[o])
            pt, pc, pb = PM.alloc(SEG)
            for lc in range(4):
                P.op("pe", lambda e, o=o, lc=lc, pt=pt, pc=pc: e.matmul(
                    pt[:, pc:pc + SEG], g2t[:, lc, o * 128:(o + 1) * 128], lora[:, lc, :], start=(lc == 0),
                    stop=(lc == 3)), [Bc, Blora], pb)
            P.op("dve", lambda e, o=o, pt=pt, pc=pc: e.tensor_tensor(out=FS["yo"][:, o, :], in0=FS["yo"][:, o, :],
                                                                    in1=pt[:, pc:pc + SEG], op=ALU.mult),
                 BRc[o] + pb, BRc[o])
            outs.append(P.dma("sp", ygT[o * 128:(o + 1) * 128, c0:c0 + SEG], FS["yo"][:, o, :], reads=BRc[o]))
        return zi

    for sg in range(nseg):
        zi = do_seg(sg, zi)
    P.finish(outs)
    P.emit()
    return nc


def rwkv_consts():
    cm = np.zeros((128, 9, 128), np.float32)
    cm[:, 0, :] = np.eye(128)
    for h in range(2):
        cm[64 * h:64 * h + 64, 1, 64 * h:64 * h + 64] = 1.0
        for s in range(64):
            cm[64 * h + s, 2, 64 * h + s + 1:64 * h + 64] = 1.0
            cm[64 * h + s, 3, 64 * h + s:64 * h + 64] = 1.0
            cm[64 * h + s, 5:9, 64 * h:64 * h + s] = 1.0
        cm[64 * h:64 * h + 64, 4, 0:64] = np.eye(64)
    rmask = np.ones((128, SEG), np.float32)
    rmask[:, 0::CH] = 0.0
    return cm, rmask


_NC_CACHE = {}


def _get(name, fn):
    if name not in _NC_CACHE:
        _NC_CACHE[name] = fn()
    return _NC_CACHE[name]


def _rwkv_inputs(I, hg):
    fs = slice(hg * HGF, (hg + 1) * HGF)
    g1p = np.zeros((D, 512), np.float32)
    g1p[:, :GL] = I['rwkv_g1'][0]
    g2p = np.zeros((512, HGF), np.float32)
    g2p[:GL] = I['rwkv_g2'][0][:, fs]

    def hv(v):
        return np.ascontiguousarray(np.asarray(v, np.float32).reshape(-1)[fs].reshape(NHP, 128).T)
    vecs = np.stack([hv(I['rwkv_w0'][0]), hv(I['rwkv_a0'][0]), hv(I['rwkv_kk'][0]), hv(I['rwkv_ka'][0]),
                     hv(I['rwkv_rk'][0]), hv(I['rwkv_lnx_g'][0]), hv(I['rwkv_lnx_b'][0])], axis=1)
    cm, rmask = rwkv_consts()
    c = np.ascontiguousarray
    return {
        "mu": fm(I['rwkv_mu'][0]),
        "wr": c(I['rwkv_wr'][0][:, fs]), "wk": c(I['rwkv_wk'][0][:, fs]), "wv": c(I['rwkv_wv'][0][:, fs]),
        "w1": c(I['rwkv_w1'][0]), "w2": c(I['rwkv_w2'][0][:, fs]),
        "a1": c(I['rwkv_a1'][0]), "a2": c(I['rwkv_a2'][0][:, fs]),
        "g1": g1p, "g2": g2p, "vecs": c(vecs), "cm": cm, "rmask": rmask,
    }


def _tile_cols(full, t0, halo, n):
    out = np.zeros((D, halo + n), np.float32)
    lo = max(0, t0 - halo)
    out[:, lo - (t0 - halo):] = full[:, lo:t0 + n]
    return out


def kernel(**inputs):
    I = {k: np.asarray(v) for k, v in inputs.items()}
    x = I['x'].astype(np.float32, copy=False)
    c = I['c'].astype(np.float32, copy=False)
    cores = list(range(NCORE))
    TPC = (B * T) // TT // NCORE
    ncA = _get("A", build_mods)
    cT = np.ascontiguousarray(c.T.reshape(NKC, 128, B).transpose(1, 0, 2))
    aw, ab = I['ada_w'], I['ada_b']
    ims = [{"cT": cT, "aw": np.ascontiguousarray(aw[:, :, i * MODC:(i + 1) * MODC]),
            "ab": np.ascontiguousarray(np.broadcast_to(ab[:, None, i * MODC:(i + 1) * MODC], (2, B, MODC)))}
           for i in cores]
    res = run_bass_kernel_spmd(ncA, ims, core_ids=cores)
    mod = np.concatenate([r["mo"] for r in res.results], axis=2)
    del ims
    ncB = _get("B", lambda: build_layer0(TPC))
    shared = {"ng": fm(I['norm_g'][0]), "pool_w": np.ascontiguousarray(I['pool_w'][0]),
              "pool_scale": fm(I['pool_scale'][0]), "w_up": np.ascontiguousarray(I['ffn_w_up'][0]),
              "conv_w": fm_ff(I['ffn_conv_w'][0]), "conv_b": fm_ff(I['ffn_conv_b'][0]),
              "w_down": np.ascontiguousarray(I['ffn_w_down'][0])}
    ims = []
    for i in cores:
        b = (i * TPC * TT) // T
        t0s = [(i * TPC + k) * TT - b * T for k in range(TPC)]
        vc = [layer0_consts(t0) for t0 in t0s]
        d = dict(shared)
        d.update({"xT_in": np.stack([tile_xT(x[b], t0, HB, TT) for t0 in t0s]),
                  "valid": np.stack([v[0] for v in vc]), "rc": np.stack([v[1] for v in vc]),
                  "mod": fm(mod[0, b].reshape(6, D))})
        ims.append(d)
    res = run_bass_kernel_spmd(ncB, ims, core_ids=cores)
    x1T = [np.concatenate([res.results[i]["x2T"][k] for i in cores if (i * TPC * TT) // T == b
                           for k in range(TPC)], axis=1) for b in range(B)]
    del ims, shared
    ncC = _get("C", lambda: build_rwkv(T // SEG))
    rw = [_rwkv_inputs(I, hg) for hg in range(NCORE // B)]
    ims = []
    for i in cores:
        b, hg = i // (NCORE // B), i % (NCORE // B)
        d = dict(rw[hg])
        d.update({"xT": x1T[b], "mod": fm(mod[1, b].reshape(6, D)), "ng": fm(I['norm_g'][1])})
        ims.append(d)
    res = run_bass_kernel_spmd(ncC, ims, core_ids=cores)
    ygT = [np.concatenate([res.results[b * (NCORE // B) + hg]["ygT"] for hg in range(NCORE // B)], axis=0)
           for b in range(B)]
    del ims, rw
    ncD = _get("D", lambda: build_layer1_tail(TPC))
    shared = {"ng": fm(I['norm_g'][1]), "fg": fm(I['final_g']), "wo": np.ascontiguousarray(I['rwkv_wo'][0]),
              "w_up": np.ascontiguousarray(I['ffn_w_up'][1]),
              "conv_w": fm_ff(I['ffn_conv_w'][1]), "conv_b": fm_ff(I['ffn_conv_b'][1]),
              "w_down": np.ascontiguousarray(I['ffn_w_down'][1])}
    ims = []
    for i in cores:
        b = (i * TPC * TT) // T
        t0s = [(i * TPC + k) * TT - b * T for k in range(TPC)]
        d = dict(shared)
        valid = np.stack([np.full((128, 2), 1.0 if t0 > 0 else 0.0, np.float32) for t0 in t0s])
        d.update({"x1in": np.stack([_tile_cols(x1T[b], t0, 2, TT) for t0 in t0s]),
                  "ygT": np.stack([_tile_cols(ygT[b], t0, 2, TT) for t0 in t0s]),
                  "valid": valid, "mod": fm(mod[1, b].reshape(6, D))})
        ims.append(d)
    res = run_bass_kernel_spmd(ncD, ims, core_ids=cores)
    out = np.empty((B, T, D), np.float32)
    for i in cores:
        b = (i * TPC * TT) // T
        for k in range(TPC):
            t0 = (i * TPC + k) * TT - b * T
            out[b, t0:t0 + TT, :] = res.results[i]["outT"][k].T
    return out
```
